# Optimizing a Trainium2 kernel written in Bass

```python
import jax, jax.numpy as jnp
from jax import lax
import numpy as np

D_MODEL = 1024
BATCH = 1
SEQ = 16384
DEPTH = 1

HEAD_DIM = 64
SB_HEADS = 8
DSA_GROUPS = ((128, 1), (512, 4), (2048, 16))
DSA_HEADS_PER_GROUP = 4
DSA_HEADS = DSA_HEADS_PER_GROUP * len(DSA_GROUPS)
MEM_HEADS = 4
MEM_LEN = 256
D_FF = 2816
ROPE_THETA = 10000.0
NORM_EPS = 1e-6
Q_BLOCK = 128
N_BRANCH = 3
SB_W = SB_HEADS * HEAD_DIM
DSA_W = DSA_HEADS * HEAD_DIM
DSA_OUT_W = DSA_HEADS_PER_GROUP * HEAD_DIM
MEM_W = MEM_HEADS * HEAD_DIM
IN_COLS = 3 * SB_W + 3 * DSA_W + MEM_W
MAX_DIL = max(r for _, r in DSA_GROUPS)

kernel_name = "hybrid_stickbreak_dilated_memory_block"

F32 = jnp.float32


def rms_norm(x, g):
    xf = x.astype(F32)
    y = xf * lax.rsqrt(jnp.mean(xf * xf, axis=-1, keepdims=True) + NORM_EPS)
    return (y * g.astype(F32)).astype(x.dtype)


def swiglu(x, w1, w3, w2):
    return (jax.nn.silu(x @ w1) * (x @ w3)) @ w2


def split_heads(t, n):
    b, s, _ = t.shape
    return t.reshape(b, s, n, HEAD_DIM).transpose(0, 2, 1, 3)


def merge_heads(t):
    b, n, s, hd = t.shape
    return t.transpose(0, 2, 1, 3).reshape(b, s, n * hd)


def rope(x, positions):
    half = HEAD_DIM // 2
    inv_freq = jnp.power(ROPE_THETA, -jnp.arange(half, dtype=F32) / half)
    ang = positions.astype(F32)[:, None] * inv_freq[None, :]
    cos, sin = jnp.cos(ang), jnp.sin(ang)
    xf = x.astype(F32)
    x1, x2 = xf[..., :half], xf[..., half:]
    return jnp.concatenate([x1 * cos - x2 * sin, x2 * cos + x1 * sin], axis=-1).astype(x.dtype)


def stick_breaking_attention(q, k, v):
    b, h, s, hd = q.shape
    nb = s // Q_BLOCK
    scale = hd ** -0.5
    qb = q.reshape(b, h, nb, Q_BLOCK, hd).transpose(2, 0, 1, 3, 4)
    key_pos = jnp.arange(s)
    vf = v.astype(F32)

    def block(args):
        qi, bi = args
        z = jnp.einsum('bhqd,bhkd->bhqk', qi, k).astype(F32) * scale
        q_pos = bi * Q_BLOCK + jnp.arange(Q_BLOCK)
        before = key_pos[None, :] < q_pos[:, None]
        log_fail = jnp.where(before, jax.nn.log_sigmoid(-z), 0.0)
        later = lax.cumsum(log_fail, axis=3, reverse=True) - log_fail
        w = jnp.where(before, jnp.exp(jax.nn.log_sigmoid(z) + later), 0.0)
        return jnp.einsum('bhqk,bhkd->bhqd', w, vf)

    out = lax.map(block, (qb, jnp.arange(nb)))
    return out.transpose(1, 2, 0, 3, 4).reshape(b, h, s, hd).astype(q.dtype)


def banded_window_attention(q, k, v, n_back):
    *lead, n, hd = q.shape
    nb = n // Q_BLOCK
    scale = hd ** -0.5
    qb = q.reshape(*lead, nb, Q_BLOCK, hd)

    def with_prev(t):
        tp = jnp.concatenate([jnp.zeros_like(t[..., :Q_BLOCK, :]), t], axis=-2)
        tp = tp.reshape(*lead, nb + 1, Q_BLOCK, hd)
        return jnp.concatenate([tp[..., :-1, :, :], tp[..., 1:, :, :]], axis=-2)

    kb, vb = with_prev(k), with_prev(v)
    sc = jnp.einsum('...qd,...kd->...qk', qb, kb).astype(F32) * scale
    qi = jnp.arange(Q_BLOCK)[:, None]
    kj = jnp.arange(2 * Q_BLOCK)[None, :]
    dist = Q_BLOCK + qi - kj
    blk = jnp.arange(nb)[:, None, None]
    valid = (dist >= 0) & (dist <= n_back) & ((blk > 0) | (kj >= Q_BLOCK))
    sc = jnp.where(valid, sc, -jnp.inf)
    m = jnp.max(sc, axis=-1, keepdims=True)
    p = jnp.exp(sc - m)
    den = jnp.sum(p, axis=-1, keepdims=True)
    out = jnp.einsum('...qk,...kd->...qd', p, vb.astype(F32)) / den
    lse = (m + jnp.log(den))[..., 0]
    return out.reshape(*lead, n, hd), lse.reshape(*lead, n)


def dilated_mixture_attention(q, k, v):
    b, _, s, hd = q.shape
    unit = Q_BLOCK * MAX_DIL
    sp = ((s + unit - 1) // unit) * unit
    pad = ((0, 0), (0, 0), (0, sp - s), (0, 0))
    q, k, v = jnp.pad(q, pad), jnp.pad(k, pad), jnp.pad(v, pad)
    outs, lses = [], []
    for g, (window, dil) in enumerate(DSA_GROUPS):
        sl = slice(g * DSA_HEADS_PER_GROUP, (g + 1) * DSA_HEADS_PER_GROUP)

        def stride_gather(t):
            return t[:, sl].reshape(b, DSA_HEADS_PER_GROUP, sp // dil, dil, hd).swapaxes(2, 3)

        o, l = banded_window_attention(stride_gather(q), stride_gather(k), stride_gather(v), window // dil)
        outs.append(o.swapaxes(2, 3).reshape(b, DSA_HEADS_PER_GROUP, sp, hd))
        lses.append(l.swapaxes(2, 3).reshape(b, DSA_HEADS_PER_GROUP, sp))
    alpha = jax.nn.softmax(jnp.stack(lses, axis=0), axis=0)
    o = jnp.sum(alpha[..., None] * jnp.stack(outs, axis=0), axis=0)
    return o[:, :, :s].astype(q.dtype)


def memory_cross_attention(q, mem_h, w_mem_kv, qn, kn):
    kv = mem_h @ w_mem_kv
    km, vm = jnp.split(kv, 2, axis=-1)
    km = rms_norm(split_heads(km, MEM_HEADS), kn)
    vm = split_heads(vm, MEM_HEADS)
    q = rms_norm(q, qn)
    sc = jnp.einsum('bhqd,bhkd->bhqk', q, km).astype(F32) * (HEAD_DIM ** -0.5)
    p = jax.nn.softmax(sc, axis=-1)
    return jnp.einsum('bhqk,bhkd->bhqd', p, vm.astype(F32)).astype(q.dtype)


def setup_inputs(seed: int = 0) -> dict:
    key = jax.random.key(seed)
    ks = iter(jax.random.split(key, 32))

    def w(shape, fan_in):
        return jax.random.normal(next(ks), (DEPTH,) + shape, F32) * (fan_in ** -0.5)

    def gain(shape):
        return 1.0 + 0.02 * jax.random.normal(next(ks), (DEPTH,) + shape, F32)

    return {
        "x": jax.random.normal(next(ks), (BATCH, SEQ, D_MODEL), F32),
        "mem": jax.random.normal(next(ks), (BATCH, MEM_LEN, D_MODEL), F32),
        "ffn1_norm": gain((D_MODEL,)),
        "ffn1_w1": w((D_MODEL, D_FF), D_MODEL),
        "ffn1_w3": w((D_MODEL, D_FF), D_MODEL),
        "ffn1_w2": w((D_FF, D_MODEL), D_FF),
        "mix_norm": gain((D_MODEL,)),
        "mem_norm": gain((D_MODEL,)),
        "w_in": w((D_MODEL, IN_COLS), D_MODEL),
        "w_mem_kv": w((D_MODEL, 2 * MEM_W), D_MODEL),
        "qn_dsa": gain((HEAD_DIM,)),
        "kn_dsa": gain((HEAD_DIM,)),
        "qn_mem": gain((HEAD_DIM,)),
        "kn_mem": gain((HEAD_DIM,)),
        "w_branch_sb": w((SB_W, D_MODEL), SB_W),
        "w_branch_dsa": w((DSA_OUT_W, D_MODEL), DSA_OUT_W),
        "w_branch_mem": w((MEM_W, D_MODEL), MEM_W),
        "w_gate": w((D_MODEL, N_BRANCH * D_MODEL), D_MODEL),
        "b_gate": 0.01 * jax.random.normal(next(ks), (DEPTH, N_BRANCH * D_MODEL), F32),
        "w_out": w((D_MODEL, D_MODEL), D_MODEL),
        "ffn2_norm": gain((D_MODEL,)),
        "ffn2_w1": w((D_MODEL, D_FF), D_MODEL),
        "ffn2_w3": w((D_MODEL, D_FF), D_MODEL),
        "ffn2_w2": w((D_FF, D_MODEL), D_FF),
    }


def reference(x, mem, ffn1_norm, ffn1_w1, ffn1_w3, ffn1_w2, mix_norm, mem_norm, w_in, w_mem_kv,
              qn_dsa, kn_dsa, qn_mem, kn_mem, w_branch_sb, w_branch_dsa, w_branch_mem,
              w_gate, b_gate, w_out, ffn2_norm, ffn2_w1, ffn2_w3, ffn2_w2):
    b, s, d = x.shape
    positions = jnp.arange(s)
    cuts = np.cumsum([SB_W, SB_W, SB_W, DSA_W, DSA_W, DSA_W])
    for l in range(DEPTH):
        x = x + 0.5 * swiglu(rms_norm(x, ffn1_norm[l]), ffn1_w1[l], ffn1_w3[l], ffn1_w2[l])

        h = rms_norm(x, mix_norm[l])
        qa, ka, va, qb, kb, vb, qc = jnp.split(h @ w_in[l], cuts, axis=-1)

        ya = stick_breaking_attention(split_heads(qa, SB_HEADS), split_heads(ka, SB_HEADS),
                                      split_heads(va, SB_HEADS))
        ya = merge_heads(ya) @ w_branch_sb[l]

        qb_h = rope(rms_norm(split_heads(qb, DSA_HEADS), qn_dsa[l]), positions)
        kb_h = rope(rms_norm(split_heads(kb, DSA_HEADS), kn_dsa[l]), positions)
        yb = dilated_mixture_attention(qb_h, kb_h, split_heads(vb, DSA_HEADS))
        yb = merge_heads(yb) @ w_branch_dsa[l]

        yc = memory_cross_attention(split_heads(qc, MEM_HEADS), rms_norm(mem, mem_norm[l]),
                                    w_mem_kv[l], qn_mem[l], kn_mem[l])
        yc = merge_heads(yc) @ w_branch_mem[l]

        gates = jax.nn.sigmoid(h @ w_gate[l] + b_gate[l]).reshape(b, s, N_BRANCH, d)
        merged = gates[:, :, 0] * ya + gates[:, :, 1] * yb + gates[:, :, 2] * yc
        x = x + merged @ w_out[l]

        x = x + 0.5 * swiglu(rms_norm(x, ffn2_norm[l]), ffn2_w1[l], ffn2_w3[l], ffn2_w2[l])
    return x
```

```python
from contextlib import ExitStack
import numpy as np
import concourse.bass as bass
import concourse.mybir as mybir
from concourse.bass_utils import run_bass_kernel_spmd

F32 = mybir.dt.float32
BF16 = mybir.dt.bfloat16
AF = mybir.ActivationFunctionType
ALU = mybir.AluOpType

NCORES = 8
S = 16384
TOK = 2048
D = 1024
DFF = 2816
NFC = 22
EPS = 1e-6
SND_ROWS = 3072

ENG_NAMES = ("pe", "act", "dve", "pool", "sp")


class Instr:
    __slots__ = ("eng", "fn", "idx", "needs_inc", "semval", "waits", "dsem", "kind", "dval", "dinc")

    def __init__(self, eng, fn, kind):
        self.eng = eng
        self.fn = fn
        self.kind = kind
        self.needs_inc = False
        self.semval = None
        self.waits = {}
        self.dsem = None
        self.dval = None
        self.dinc = 16


class Prog:
    def __init__(self, nc):
        self.nc = nc
        self.streams = {e: [] for e in ENG_NAMES}
        self.last_w = {}
        self.readers = {}
        self.dsem_count = {}
        self.dsem_keys = []
        self.last_dma = {}
        self.barrier_deps = []
        self._pid = {}
        self.high_water = {e: {} for e in ENG_NAMES}

    def pid(self, e):
        k = id(e)
        if k not in self._pid:
            self._pid[k] = e.partition_id()
        return self._pid[k]

    def val(self, e, name):
        k = (id(e), name)
        if k not in self._pid:
            pid = self.pid(e)
            if name == "pm":
                v = (pid + 7) % 8
            elif name == "own":
                v = pid * SND_ROWS
            elif name == "prev":
                v = self.val(e, "pm") * SND_ROWS
            elif name == "p64":
                v = pid * 64
            elif name == "ptok":
                v = pid * TOK
            self._pid[k] = v
        return self._pid[k]

    def _dep(self, c, p):
        if p is None or p is c:
            return
        if p.kind == "dma":
            key = ("d", p.dsem)
            prev = c.waits.get(key)
            if prev is None or prev.dval < p.dval:
                c.waits[key] = p
        else:
            if p.eng == "pe" and c.eng == "pe" and c.kind == "op":
                return
            key = ("e", p.eng)
            prev = c.waits.get(key)
            if prev is None or prev.idx < p.idx:
                c.waits[key] = p

    def add(self, eng, fn, reads=(), writes=(), kind="op", dkey=None, dinc=16):
        ins = Instr(eng, fn, kind)
        ins.idx = len(self.streams[eng])
        if kind == "dma":
            if dkey is None:
                dkey = writes[0]
            if dkey not in self.dsem_count:
                self.dsem_count[dkey] = 0
                self.dsem_keys.append(dkey)
            self.dsem_count[dkey] += dinc
            ins.dsem = dkey
            ins.dinc = dinc
            ins.dval = self.dsem_count[dkey]
            self.last_dma[dkey] = ins
        for p in self.barrier_deps:
            self._dep(ins, p)
        for r in reads:
            self._dep(ins, self.last_w.get(r))
        for w in writes:
            self._dep(ins, self.last_w.get(w))
            for rd in self.readers.get(w, ()):
                self._dep(ins, rd)
        hw = self.high_water[eng]
        for key in list(ins.waits.keys()):
            p = ins.waits[key]
            v = p.dval if key[0] == "d" else p.idx
            if hw.get(key, -1) >= v:
                del ins.waits[key]
            else:
                hw[key] = v
                if key[0] == "e":
                    p.needs_inc = True
        for r in reads:
            lst = self.readers.setdefault(r, [])
            lst[:] = [x for x in lst if not (x.kind == "op" and ins.kind == "op" and x.eng == ins.eng)]
            lst.append(ins)
        for w in writes:
            self.last_w[w] = ins
            self.readers[w] = []
        self.streams[eng].append(ins)
        return ins

    def dma(self, eng, out, in_, reads=(), writes=(), dkey=None):
        def fn(e, out=out, in_=in_):
            return e.dma_start(out=out, in_=in_)
        return self.add(eng, fn, reads, writes, kind="dma", dkey=dkey)

    def barrier(self):
        deps = []
        for e in ENG_NAMES:
            ops = [i for i in self.streams[e] if i.kind == "op"]
            if ops:
                ops[-1].needs_inc = True
                deps.append(ops[-1])
        deps.extend(self.last_dma.values())
        self.barrier_deps = deps

    def emit(self, stack):
        nc = self.nc
        esem = {e: stack.enter_context(nc.semaphore("s_" + e)) for e in ENG_NAMES}
        dsem = {k: stack.enter_context(nc.semaphore("d%d" % i)) for i, k in enumerate(self.dsem_keys)}
        for e in ENG_NAMES:
            c = 0
            for ins in self.streams[e]:
                if ins.kind == "op" and ins.needs_inc:
                    c += 1
                    ins.semval = c
        block = stack.enter_context(nc.Block())
        binder = {"pe": block.tensor, "act": block.scalar, "dve": block.vector,
                  "pool": block.gpsimd, "sp": block.sync}
        for e in ENG_NAMES:
            def body(engine, e=e):
                waited = {}
                for ins in self.streams[e]:
                    for key, p in ins.waits.items():
                        if key[0] == "d":
                            v = p.dval
                            s = dsem[p.dsem]
                        else:
                            v = p.semval
                            s = esem[p.eng]
                        if waited.get(key, 0) < v:
                            engine.wait_ge(s, v)
                            waited[key] = v
                    r = ins.fn(engine)
                    if ins.kind == "dma":
                        r.then_inc(dsem[ins.dsem], ins.dinc)
                    elif ins.needs_inc:
                        r.then_inc(esem[e], 1)
                if e == "sp":
                    for k in self.dsem_keys:
                        engine.wait_ge(dsem[k], self.dsem_count[k])
            binder[e](body)


C_ONESM, C_ONES, C_BD1, C_BD64, C_NTRI, C_NONES, C_ID, C_MTRI, C_MB = 0, 128, 256, 384, 512, 640, 768, 896, 1024
C_TOT = 2048


def _consts(core):
    c = np.zeros((128, C_TOT), np.float32)
    c[:, C_ONESM:C_ONESM + 128] = 1.0 / 1024.0
    c[:, C_ONES:C_ONES + 128] = 1.0
    bd = np.zeros((128, 128), np.float32)
    bd[:64, :64] = 1.0
    bd[64:, 64:] = 1.0
    c[:, C_BD1:C_BD1 + 128] = bd
    c[:, C_BD64:C_BD64 + 128] = bd / 64.0
    j = np.arange(128)[:, None]
    s = np.arange(128)[None, :]
    c[:, C_NTRI:C_NTRI + 128] = -(j >= s).astype(np.float32)
    c[:, C_NONES:C_NONES + 128] = -1.0
    c[:, C_ID:C_ID + 128] = np.eye(128, dtype=np.float32)
    c[:, C_MTRI:C_MTRI + 128] = (s > j).astype(np.float32)
    cur = (s >= j).astype(np.float32)
    prv = (j >= s).astype(np.float32)
    mb0 = np.zeros((128, 1024), np.float32)
    for k in range(8):
        isprev = (k % 4) >= 2
        c[:, C_MB + 128 * k:C_MB + 128 * (k + 1)] = prv if isprev else cur
        mb0[:, 128 * k:128 * (k + 1)] = (0.0 if core == 0 else prv) if isprev else cur
    rt = np.zeros((128, 128), np.float32)
    for base in (0, 64):
        for d in range(32):
            rt[base + d + 32, base + d] = -1.0
            rt[base + d, base + d + 32] = 1.0
    half = 32
    inv_freq = np.power(np.float32(10000.0), -np.arange(half, dtype=np.float32) / np.float32(half)).astype(np.float32)
    pos = (np.arange(TOK) + core * TOK).astype(np.float32)
    ang = (pos[:, None] * inv_freq[None, :]).astype(np.float32)
    cos = np.cos(ang).astype(np.float32).T
    sin = np.sin(ang).astype(np.float32).T
    rope = np.zeros((2, 128, TOK), np.float32)
    for p in range(128):
        rope[0, p] = cos[(p % 64) % 32]
        rope[1, p] = sin[(p % 64) % 32]
    return c, mb0, rt, rope


def build(stage=99, bmax=10**9, p3level=3):
    nc = bass.Bass("TRN2", target_bir_lowering=False)

    def din(name, shape, dt=F32):
        return nc.dram_tensor(name, list(shape), dt, kind="ExternalInput").ap()

    xT_d = din("xT", [D, TOK])
    memT_d = din("memT", [D, 256])
    w1a, w3a, w2a = din("f1w1", [D, DFF]), din("f1w3", [D, DFF]), din("f1w2", [DFF, D])
    w1b, w3b, w2b = din("f2w1", [D, DFF]), din("f2w3", [D, DFF]), din("f2w2", [DFF, D])
    win_d = din("w_in", [D, 4096])
    wkv_d = din("w_mem_kv", [D, 512])
    wsb_d, wdsa_d, wmem_d = din("w_sb", [512, D]), din("w_dsa", [256, D]), din("w_mem", [256, D])
    wg_d = din("w_gate", [D, 3072])
    wo_d = din("w_out", [D, D])
    vecs_d = din("vecs", [128, 64])
    cst_d = din("cst", [128, C_TOT])
    mb0_d = din("mb0", [128, 1024])
    rt_d = din("rt", [128, 128])
    rope_d = din("rope", [2, 128, TOK])
    outT_d = nc.dram_tensor("outT", [D, TOK], F32, kind="ExternalOutput").ap()

    snd = nc.dram_tensor("snd", [SND_ROWS, 2048], BF16)
    gat = nc.dram_tensor("gat", [NCORES * SND_ROWS, 2048], BF16)
    qbs = nc.dram_tensor("qbs", [768, TOK], BF16)
    ybs = nc.dram_tensor("ybs", [256, TOK], BF16)
    ycs = nc.dram_tensor("ycs", [256, TOK], BF16)
    yas = nc.dram_tensor("yas", [64, S], BF16)
    yag = nc.dram_tensor("yag", [512, S], BF16)
    snd2 = snd.ap()
    gat2 = gat.ap()

    VEC_F1, VEC_MIX, VEC_MEM, VEC_F2, VEC_BG, VEC_QND, VEC_KND, VEC_QNM, VEC_KNM = 0, 8, 16, 24, 32, 56, 57, 58, 59

    with ExitStack() as st:
        P = Prog(nc)

        sbctr = [0]

        def sb(stack, name, shape, dt):
            sbctr[0] += 1
            return stack.enter_context(nc.sbuf_tensor("sb_%s_%d" % (name, sbctr[0]), list(shape), dt))

        ps = st.enter_context(nc.psum_tensor("ps", [128, 4096], F32))
        bank_ctr = [0]

        def bank():
            b = bank_ctr[0] % 8
            bank_ctr[0] += 1
            return b

        def psb(b, lo=0, hi=512, p0=0, p1=128):
            return ps[p0:p1, b * 512 + lo:b * 512 + hi]

        xT = sb(st, "xT", [128, 8, TOK], F32)
        vecs = sb(st, "vecs", [128, 64], F32)
        cst = sb(st, "cstb", [128, C_TOT], BF16)
        mb0 = sb(st, "mb0b", [128, 1024], BF16)
        rt = sb(st, "rtf", [128, 128], F32)

        def cs(off, n=128, p0=0, p1=128):
            return cst[p0:p1, off:off + n]

        for kc in range(8):
            P.dma("sp", xT[:, kc, :], xT_d[kc * 128:(kc + 1) * 128, :], writes=["x%d" % t for t in range(4)] if kc == 0 else [],
                  reads=[], dkey="xload")
        P.dma("sp", vecs[:], vecs_d, writes=["vecs"])
        P.dma("sp", rt[:], rt_d, writes=["rt"])
        P.dma("pool", cst[:], cst_d, writes=["cst"])
        P.dma("pool", mb0[:], mb0_d, writes=["mb0"])
        xload_last = P.last_dma["xload"]
        for t in range(4):
            P.last_w["x%d" % t] = xload_last

        def mm(out, lhsT, rhs, start, stop, reads, writes, sgc=False):
            P.add("pe", lambda e, out=out, lhsT=lhsT, rhs=rhs, start=start, stop=stop, sgc=sgc:
                  e.matmul(out, lhsT=lhsT, rhs=rhs, start=start, stop=stop, skip_group_check=sgc), reads=reads, writes=writes)

        def act(out, in_, func, reads, writes, scale=1.0, bias=0.0):
            P.add("act", lambda e, out=out, in_=in_, func=func, scale=scale, bias=bias:
                  e.activation(out=out, in_=in_, func=func, scale=scale, bias=bias), reads=reads, writes=writes)

        def tt(eng, out, in0, in1, op, reads, writes):
            P.add(eng, lambda e, out=out, in0=in0, in1=in1, op=op:
                  e.tensor_tensor(out=out, in0=in0, in1=in1, op=op), reads=reads, writes=writes)

        def stt(out, in0, scalar, in1, op0, op1, reads, writes):
            P.add("dve", lambda e, out=out, in0=in0, scalar=scalar, in1=in1, op0=op0, op1=op1:
                  e.scalar_tensor_tensor(out=out, in0=in0, scalar=scalar, in1=in1, op0=op0, op1=op1),
                  reads=reads, writes=writes)

        def recip(out, in_, reads, writes):
            P.add("dve", lambda e, out=out, in_=in_: e.reciprocal(out=out, in_=in_), reads=reads, writes=writes)

        def copy(eng, out, in_, reads, writes):
            P.add(eng, lambda e, out=out, in_=in_: e.tensor_copy(out=out, in_=in_), reads=reads, writes=writes)

        def memset(eng, ap, val, writes):
            P.add(eng, lambda e, ap=ap, val=val: e.memset(ap, val), writes=writes)

        def phase13(ph):
            with ExitStack() as s1:
                NW = 4 if ph == 1 else 3
                wsl = [sb(s1, "wsl%d" % i, [128, 8, 512], BF16) for i in range(NW)]
                w2sl = [sb(s1, "w2sl%d" % i, [128, 4, 1024], BF16) for i in range(2)]
                hT = sb(s1, "hT", [128, 8, 512], BF16)
                gT = sb(s1, "gT", [128, NFC, 512], BF16)
                sq = sb(s1, "sq", [128, 8, 512], BF16)
                f32s = [sb(s1, "f32s%d" % i, [128, 512], F32) for i in range(6)]
                stg = [sb(s1, "stg%d" % i, [128, 512], BF16) for i in range(4)] if ph == 1 else []
                ctr = {"w": 0, "w2": 0, "f": 0, "s": 0}

                def wslot():
                    i = ctr["w"] % NW
                    ctr["w"] += 1
                    return wsl[i], "wsl%d" % i

                def w2slot():
                    i = ctr["w2"] % 2
                    ctr["w2"] += 1
                    return w2sl[i], "w2sl%d" % i

                def fslot():
                    i = ctr["f"] % 6
                    ctr["f"] += 1
                    return f32s[i], "f32s%d" % i

                def sslot():
                    i = ctr["s"] % 4
                    ctr["s"] += 1
                    return stg[i], "stg%d" % i

                def load_w(wd, c0, ncol, nk=8):
                    slot, key = wslot()
                    src = wd.rearrange("(kc p) n -> p kc n", p=128)[:, 0:nk, c0:c0 + ncol]
                    P.dma("pool", slot[:, 0:nk, 0:ncol], src, writes=[key])
                    return slot, key

                def rmsnorm(src3, src_keys, gcol, ntok, dst, dst_key):
                    act(sq[:, :, 0:ntok], src3, AF.Square, reads=src_keys, writes=["sq"])
                    b = bank()
                    for kc in range(8):
                        mm(psb(b, 0, ntok), cs(C_ONESM), sq[:, kc, 0:ntok], kc == 0, kc == 7,
                           reads=["sq", "cst"], writes=["ps%d" % b])
                    rs, rk = fslot()
                    act(rs[:, 0:ntok], psb(b, 0, ntok), AF.Sqrt, reads=["ps%d" % b], writes=[rk], bias=EPS)
                    rstd, rk2 = fslot()
                    recip(rstd[:, 0:ntok], rs[:, 0:ntok], reads=[rk], writes=[rk2])
                    for kc in range(8):
                        stt(dst[:, kc, 0:ntok], src3[:, kc, :], vecs[:, gcol + kc:gcol + kc + 1], rstd[:, 0:ntok],
                            ALU.mult, ALU.mult, reads=src_keys + [rk2, "vecs"], writes=[dst_key])

                def ffn(T, w1d, w3d, w2d, gcol):
                    ts = slice(512 * T, 512 * T + 512)
                    xk = "x%d" % T
                    rmsnorm(xT[:, :, ts], [xk], gcol, 512, hT, "hT")
                    for fb in range(6):
                        c0 = 512 * fb
                        ncol = min(512, DFF - c0)
                        s1_, k1 = load_w(w1d, c0, ncol)
                        s3_, k3 = load_w(w3d, c0, ncol)
                        for fc in range(ncol // 128):
                            f = fb * 4 + fc
                            bu, bv = bank(), bank()
                            for kc in range(8):
                                mm(psb(bu), s1_[:, kc, fc * 128:(fc + 1) * 128], hT[:, kc, :], kc == 0, kc == 7,
                                   reads=[k1, "hT"], writes=["ps%d" % bu])
                            for kc in range(8):
                                mm(psb(bv), s3_[:, kc, fc * 128:(fc + 1) * 128], hT[:, kc, :], kc == 0, kc == 7,
                                   reads=[k3, "hT"], writes=["ps%d" % bv])
                            sl, sk = fslot()
                            act(sl[:], psb(bu), AF.Silu, reads=["ps%d" % bu], writes=[sk])
                            tt("dve", gT[:, f, :], sl[:], psb(bv), ALU.mult, reads=[sk, "ps%d" % bv], writes=["gT%d" % f])
                    f = 0
                    while f < NFC:
                        n = min(4, NFC - f)
                        slot, key = w2slot()
                        src = w2d.rearrange("(fc p) n -> p fc n", p=128)[:, f:f + n, :]
                        P.dma("pool", slot[:, 0:n, :], src, writes=[key])
                        for j in range(n):
                            for dmc in range(8):
                                mm(psb(dmc), slot[:, j, dmc * 128:(dmc + 1) * 128], gT[:, f + j, :], f + j == 0, f + j == NFC - 1,
                                   reads=[key, "gT%d" % (f + j)], writes=["ps%d" % dmc])
                        f += n
                    for dmc in range(8):
                        stt(xT[:, dmc, ts], psb(dmc), 0.5, xT[:, dmc, ts], ALU.mult, ALU.add,
                            reads=["ps%d" % dmc, xk], writes=[xk])

                def proj_fm(slot, key, co, hkey="hT"):
                    b = bank()
                    for kc in range(8):
                        mm(psb(b), slot[:, kc, co:co + 128], hT[:, kc, :], kc == 0, kc == 7,
                           reads=[key, hkey], writes=["ps%d" % b])
                    return b

                def headnorm(b, ntok, qscale, gcolv):
                    s_, sk_ = sslot()
                    act(s_[:, 0:ntok], psb(b, 0, ntok), AF.Square, reads=["ps%d" % b], writes=[sk_])
                    b2 = bank()
                    mm(psb(b2, 0, ntok), cs(C_BD1 if qscale else C_BD64), s_[:, 0:ntok], True, True,
                       reads=[sk_, "cst"], writes=["ps%d" % b2])
                    rs, rk = fslot()
                    act(rs[:, 0:ntok], psb(b2, 0, ntok), AF.Sqrt, reads=["ps%d" % b2], writes=[rk],
                        bias=(64.0 * EPS if qscale else EPS))
                    rstd, rk2 = fslot()
                    recip(rstd[:, 0:ntok], rs[:, 0:ntok], reads=[rk], writes=[rk2])
                    qn, qk = fslot()
                    stt(qn[:, 0:ntok], psb(b, 0, ntok), vecs[:, gcolv:gcolv + 1], rstd[:, 0:ntok], ALU.mult, ALU.mult,
                        reads=["ps%d" % b, rk2, "vecs"], writes=[qk])
                    return qn, qk

                def rope_to(qn, qk, rp, dst, dkey):
                    b3 = bank()
                    mm(psb(b3), rt[:], qn[:], True, True, reads=[qk, "rt"], writes=["ps%d" % b3])
                    t1, k1 = fslot()
                    tt("pool", t1[:], qn[:], rp[:, 0, :], ALU.mult, reads=[qk, "rope"], writes=[k1])
                    t2, k2 = fslot()
                    tt("dve", t2[:], psb(b3), rp[:, 1, :], ALU.mult, reads=["ps%d" % b3, "rope"], writes=[k2])
                    tt("pool", dst, t1[:], t2[:], ALU.add, reads=[k1, k2], writes=[dkey])

                if ph == 1:
                    rp = sb(s1, "rp", [128, 2, 512], F32)
                    vb3T = sb(s1, "vb3T", [128, 2, TOK], BF16)
                    v3stg = sb(s1, "v3stg", [128, 16, 256], BF16)
                    qcT = sb(s1, "qcT", [128, 2, 512], BF16)
                    KmT = sb(s1, "KmT", [128, 2, 256], BF16)
                    Vm = sb(s1, "Vm", [128, 2, 256], BF16)
                    memT = sb(s1, "memTs", [128, 8, 256], F32)
                    pTs = [sb(s1, "pTs%d" % i, [128, 512], BF16) for i in range(2)]

                    for kc in range(8):
                        P.dma("sp", memT[:, kc, :], memT_d[kc * 128:(kc + 1) * 128, :], writes=["memT"], dkey="memT")
                    rmsnorm(memT[:, :, :], ["memT"], VEC_MEM, 256, hT, "hT")
                    wkv, wkvk = load_w(wkv_d, 0, 512)
                    for ch in range(2):
                        b = bank()
                        for kc in range(8):
                            mm(psb(b, 0, 256), wkv[:, kc, ch * 128:(ch + 1) * 128], hT[:, kc, 0:256], kc == 0, kc == 7,
                               reads=[wkvk, "hT"], writes=["ps%d" % b])
                        qn, qk = headnorm(b, 256, False, VEC_KNM)
                        copy("dve", KmT[:, ch, :], qn[:, 0:256], reads=[qk], writes=["KmT"])
                    for mb in range(2):
                        b = bank()
                        for kc in range(8):
                            mm(psb(b, 0, 256), hT[:, kc, mb * 128:(mb + 1) * 128], wkv[:, kc, 256:512], kc == 0, kc == 7,
                               reads=[wkvk, "hT"], writes=["ps%d" % b])
                        copy("dve", Vm[:, mb, :], psb(b, 0, 256), reads=["ps%d" % b], writes=["Vm"])

                    vav = snd2[1792:2304, :].rearrange("(h a) (b d) -> h (a b) d", a=64, d=64)
                    vb1v = snd2[2304:2560, :].rearrange("(p a) (b f) -> p (a b) f", a=2, f=256)
                    vb2v = snd2[2560:2816, :].rearrange("(p a) (b f) -> p (a b) f", a=2, f=256)
                    vb3v = snd2[2816:3072, :].rearrange("(p a) (b f) -> p (a b) f", a=2, f=256)

                    for T in range(4):
                        ts = slice(512 * T, 512 * T + 512)
                        xk = "x%d" % T
                        ffn(T, w1a, w3a, w2a, VEC_F1)
                        if stage < 2:
                            continue
                        rmsnorm(xT[:, :, ts], [xk], VEC_MIX, 512, hT, "hT")
                        P.dma("sp", rp[:], rope_d[:, :, ts].rearrange("c p t -> p c t"), writes=["rope"])
                        for blk in range(8):
                            slot, key = load_w(win_d, 512 * blk, 512)
                            if blk in (0, 1):
                                for j in range(4):
                                    b = proj_fm(slot, key, 128 * j)
                                    s_, sk_ = sslot()
                                    act(s_[:], psb(b), AF.Copy, reads=["ps%d" % b], writes=[sk_],
                                        scale=(0.125 if blk == 0 else 1.0))
                                    r0 = 512 * blk + 128 * j
                                    P.dma("sp", snd2[r0:r0 + 128, ts], s_[:], reads=[sk_], writes=["snd"], dkey="snd_w")
                            elif blk == 2:
                                for tb in range(4):
                                    b = bank()
                                    for kc in range(8):
                                        mm(psb(b), hT[:, kc, tb * 128:(tb + 1) * 128], slot[:, kc, :], kc == 0, kc == 7,
                                           reads=[key, "hT"], writes=["ps%d" % b])
                                    s_, sk_ = sslot()
                                    copy("dve", s_[:], psb(b), reads=["ps%d" % b], writes=[sk_])
                                    t0 = 512 * T + 128 * tb
                                    P.dma("sp", vav[:, t0:t0 + 128, :].rearrange("h t d -> t h d"),
                                          s_[:].rearrange("p (h d) -> p h d", d=64), reads=[sk_], writes=["snd"], dkey="snd_w")
                            elif blk in (3, 4, 5):
                                for j in range(4):
                                    col = 512 * blk + 128 * j
                                    isq = col < 2304
                                    ch = (col - 1536) // 128 if isq else (col - 2304) // 128
                                    b = proj_fm(slot, key, 128 * j)
                                    qn, qk = headnorm(b, 512, isq, VEC_QND if isq else VEC_KND)
                                    s_, sk_ = sslot()
                                    rope_to(qn, qk, rp, s_[:], sk_)
                                    if isq:
                                        P.dma("sp", qbs.ap()[128 * ch:128 * ch + 128, ts], s_[:], reads=[sk_], writes=["qbs"], dkey="qbs_w")
                                    else:
                                        r0 = 1024 + 128 * ch
                                        P.dma("sp", snd2[r0:r0 + 128, ts], s_[:], reads=[sk_], writes=["snd"], dkey="snd_w")
                            elif blk == 6:
                                for tb in range(4):
                                    b = bank()
                                    for kc in range(8):
                                        mm(psb(b, 0, 256), hT[:, kc, tb * 128:(tb + 1) * 128], slot[:, kc, 0:256], kc == 0, kc == 7,
                                           reads=[key, "hT"], writes=["ps%d" % b])
                                    s_, sk_ = sslot()
                                    copy("dve", s_[:, 0:256], psb(b, 0, 256), reads=["ps%d" % b], writes=[sk_])
                                    P.dma("sp", vb1v[:, 4 * T + tb, :], s_[:, 0:256], reads=[sk_], writes=["snd"], dkey="snd_w")
                                for c in range(4):
                                    b = bank()
                                    for kc in range(8):
                                        mm(psb(b, 0, 256), hT[:, kc, c:512:4], slot[:, kc, 256:512], kc == 0, kc == 7,
                                           reads=[key, "hT"], writes=["ps%d" % b])
                                    s_, sk_ = sslot()
                                    copy("dve", s_[:, 0:256], psb(b, 0, 256), reads=["ps%d" % b], writes=[sk_])
                                    P.dma("sp", vb2v[:, 4 * T + c, :], s_[:, 0:256], reads=[sk_], writes=["snd"], dkey="snd_w")
                            else:
                                for j in range(2):
                                    b = proj_fm(slot, key, 128 * j)
                                    copy("dve", vb3T[:, j, ts], psb(b), reads=["ps%d" % b], writes=["vb3T"])
                                for j in range(2):
                                    b = proj_fm(slot, key, 256 + 128 * j)
                                    qn, qk = headnorm(b, 512, True, VEC_QNM)
                                    copy("dve", qcT[:, j, :], qn[:], reads=[qk], writes=["qcT"])
                        for hh in range(4):
                            ch, p0 = hh // 2, 64 * (hh % 2)
                            pk = []
                            for mb in range(2):
                                b = bank()
                                mm(psb(b), KmT[p0:p0 + 64, ch, mb * 128:(mb + 1) * 128], qcT[p0:p0 + 64, ch, :], True, True,
                                   reads=["KmT", "qcT"], writes=["ps%d" % b])
                                act(pTs[mb][:], psb(b), AF.Exp, reads=["ps%d" % b], writes=["pTs%d" % mb])
                            bn, bd_ = bank(), bank()
                            for mb in range(2):
                                mm(psb(bn, 0, 512, 0, 64), Vm[:, mb, hh * 64:(hh + 1) * 64], pTs[mb][:], mb == 0, mb == 1,
                                   reads=["Vm", "pTs%d" % mb], writes=["ps%d" % bn])
                            for mb in range(2):
                                mm(psb(bd_, 0, 512, 0, 64), cs(C_ONES, 64), pTs[mb][:], mb == 0, mb == 1,
                                   reads=["cst", "pTs%d" % mb], writes=["ps%d" % bd_])
                            rd, rk = fslot()
                            recip(rd[0:64, :], psb(bd_, 0, 512, 0, 64), reads=["ps%d" % bd_], writes=[rk])
                            s_, sk_ = sslot()
                            tt("dve", s_[0:64, :], psb(bn, 0, 512, 0, 64), rd[0:64, :], ALU.mult, reads=["ps%d" % bn, rk], writes=[sk_])
                            P.dma("sp", ycs.ap()[64 * hh:64 * hh + 64, ts], s_[0:64, :], reads=[sk_], writes=["ycs"], dkey="ycs_w")
                    if stage >= 2:
                        for c in range(16):
                            b = bank()
                            for ch in range(2):
                                mm(psb(b, ch * 128, (ch + 1) * 128), vb3T[:, ch, c:TOK:16], cs(C_ID), ch == 0, ch == 1,
                                   reads=["vb3T", "cst"], writes=["ps%d" % b])
                            copy("dve", v3stg[:, c, :], psb(b, 0, 256), reads=["ps%d" % b], writes=["v3stg"])
                        P.dma("sp", vb3v, v3stg[:], reads=["v3stg"], writes=["snd"], dkey="snd_w")

                if ph == 3:
                    yaT = sb(s1, "yaT", [128, 4, 512], BF16)
                    ybT = sb(s1, "ybT", [128, 2, 512], BF16)
                    ycT = sb(s1, "ycT", [128, 2, 512], BF16)
                    mT = sb(s1, "mT", [128, 8, 512], BF16)
                    wbr = sb(s1, "wbr", [128, 8, 1024], BF16)
                    gs = [sb(s1, "gs%d" % i, [128, 512], F32) for i in range(3)]
                    P.dma("pool", wbr[:, 0:4, :], wsb_d.rearrange("(fc p) n -> p fc n", p=128), writes=["wbr"], dkey="wbr")
                    P.dma("pool", wbr[:, 4:6, :], wdsa_d.rearrange("(fc p) n -> p fc n", p=128), writes=["wbr"], dkey="wbr")
                    P.dma("pool", wbr[:, 6:8, :], wmem_d.rearrange("(fc p) n -> p fc n", p=128), writes=["wbr"], dkey="wbr")
                    for T in range(4):
                        ts = slice(512 * T, 512 * T + 512)
                        xk = "x%d" % T

                        def ld_ya(e, T=T):
                            src = yag.ap()[:, bass.ds(P.val(e, "ptok") + 512 * T, 512)].rearrange("(ch p) t -> p ch t", p=128)
                            return e.dma_start(out=yaT[:], in_=src)
                        P.add("act", ld_ya, reads=["yag"], writes=["yaT"], kind="dma", dkey="yaT")
                        P.dma("sp", ybT[:], ybs.ap()[:, ts].rearrange("(ch p) t -> p ch t", p=128), reads=["ybs"], writes=["ybT"])
                        P.dma("sp", ycT[:], ycs.ap()[:, ts].rearrange("(ch p) t -> p ch t", p=128), reads=["ycs"], writes=["ycT"])
                        rmsnorm(xT[:, :, ts], [xk], VEC_MIX, 512, hT, "hT")
                        for half in range(2):
                            gslots = [load_w(wg_d, br * 1024 + half * 512, 512) for br in range(3)]
                            for j in range(4):
                                dmc = half * 4 + j
                                gb = []
                                for br in range(3):
                                    b = proj_fm(gslots[br][0], gslots[br][1], 128 * j)
                                    col = VEC_BG + br * 8 + dmc
                                    P.add("act", lambda e, o=gs[br][:], i=psb(b), bcol=col:
                                          e.activation(out=o, in_=i, func=AF.Sigmoid, bias=vecs[:, bcol:bcol + 1], scale=1.0),
                                          reads=["ps%d" % b, "vecs"], writes=["gs%d" % br])
                                pb = []
                                specs = ((0, 4, yaT, "yaT"), (4, 2, ybT, "ybT"), (6, 2, ycT, "ycT"))
                                for (w0, nk, yt, yk) in specs:
                                    b = bank()
                                    for k in range(nk):
                                        mm(psb(b), wbr[:, w0 + k, dmc * 128:(dmc + 1) * 128], yt[:, k, :], k == 0, k == nk - 1,
                                           reads=["wbr", yk], writes=["ps%d" % b])
                                    pb.append(b)
                                t1, k1 = fslot()
                                tt("dve", t1[:], gs[0][:], psb(pb[0]), ALU.mult, reads=["gs0", "ps%d" % pb[0]], writes=[k1])
                                t2, k2 = fslot()
                                tt("dve", t2[:], gs[1][:], psb(pb[1]), ALU.mult, reads=["gs1", "ps%d" % pb[1]], writes=[k2])
                                t3, k3 = fslot()
                                tt("dve", t3[:], gs[2][:], psb(pb[2]), ALU.mult, reads=["gs2", "ps%d" % pb[2]], writes=[k3])
                                tt("pool", t1[:], t1[:], t2[:], ALU.add, reads=[k1, k2], writes=[k1])
                                tt("pool", mT[:, dmc, :], t1[:], t3[:], ALU.add, reads=[k1, k3], writes=["mT%d" % dmc])
                        for half in range(2 if p3level >= 2 else 0):
                            slot, key = load_w(wo_d, half * 512, 512)
                            for j in range(4):
                                dmc = half * 4 + j
                                b = bank()
                                for kc in range(8):
                                    mm(psb(b), slot[:, kc, j * 128:(j + 1) * 128], mT[:, kc, :], kc == 0, kc == 7,
                                       reads=[key, "mT%d" % kc], writes=["ps%d" % b])
                                tt("dve", xT[:, dmc, ts], psb(b), xT[:, dmc, ts], ALU.add, reads=["ps%d" % b, xk], writes=[xk])
                        if p3level >= 3:
                            ffn(T, w1b, w3b, w2b, VEC_F2)
                        for kc in range(8):
                            P.dma("sp", outT_d[kc * 128:(kc + 1) * 128, ts], xT[:, kc, ts], reads=[xk], writes=["outT"], dkey="out_w")

        def phase_b():
            with ExitStack() as s2:
                qT = sb(s2, "bqT", [128, 2, TOK], BF16)
                kT = sb(s2, "bkT", [128, 2, 2 * TOK], BF16)
                vB = sb(s2, "bv", [128, 33, 256], BF16)
                accN = sb(s2, "accN", [64, 4, TOK], F32)
                accD = sb(s2, "accD", [64, 4, TOK], F32)
                eS = [sb(s2, "beS%d" % i, [128, 1024], F32) for i in range(2)]
                pT = [sb(s2, "bpT%d" % i, [128, 1024], BF16) for i in range(2)]
                memset("pool", accN[:], 0.0, writes=["accN"])
                memset("pool", accD[:], 0.0, writes=["accD"])
                ucount = 0
                for g, r in enumerate((1, 4, 16)):
                    H = 128 * r
                    P.dma("sp", qT[:], qbs.ap()[256 * g:256 * g + 256, :].rearrange("(ch p) t -> p ch t", p=128),
                          reads=["qbs"], writes=["bqT"])

                    def ld_k(e, g=g, H=H):
                        src = gat2[bass.ds(P.val(e, "own") + (1024 + 256 * g), 256), :].rearrange("(ch p) t -> p ch t", p=128)
                        return e.dma_start(out=kT[:, :, H:H + TOK], in_=src)

                    def ld_kh(e, g=g, H=H):
                        src = gat2[bass.ds(P.val(e, "prev") + (1024 + 256 * g), 256), :].rearrange("(ch p) t -> p ch t", p=128)
                        return e.dma_start(out=kT[:, :, 0:H], in_=src[:, :, TOK - H:TOK])
                    P.add("sp", ld_k, reads=["gat"], writes=["bkT"], kind="dma", dkey="bkT")
                    P.add("sp", ld_kh, reads=["gat"], writes=["bkT"], kind="dma", dkey="bkT")
                    nh = {0: 1, 1: 4, 2: 16}[g]
                    vr0 = 2304 + 256 * g

                    def ld_v(e, vr0=vr0, nh=nh):
                        src = gat2[bass.ds(P.val(e, "own") + vr0, 256), :].rearrange("(p a) (b f) -> p (a b) f", a=2, f=256)
                        return e.dma_start(out=vB[:, nh:nh + 16, :], in_=src)

                    def ld_vh(e, vr0=vr0, nh=nh):
                        src = gat2[bass.ds(P.val(e, "prev") + vr0, 256), :].rearrange("(p a) (b f) -> p (a b) f", a=2, f=256)
                        return e.dma_start(out=vB[:, 0:nh, :], in_=src[:, 16 - nh:16, :])
                    P.add("act", ld_v, reads=["gat"], writes=["bv"], kind="dma", dkey="bv")
                    P.add("act", ld_vh, reads=["gat"], writes=["bv"], kind="dma", dkey="bv")
                    nres = r
                    nblk = 16 // r
                    for c in range(nres):
                        for bq in range(nblk):
                            u = ucount % 2
                            ucount += 1
                            q_lo = c + 128 * r * bq
                            qsl = slice(q_lo, q_lo + 127 * r + 1, r)
                            kcur = slice(H + q_lo, H + q_lo + 127 * r + 1, r)
                            kprv = slice(q_lo, q_lo + 127 * r + 1, r)
                            if g == 0:
                                vcur, vprv = nh + bq, nh + bq - 1
                            elif g == 1:
                                vcur, vprv = nh + 4 * bq + c, nh + 4 * (bq - 1) + c
                            else:
                                vcur, vprv = nh + c, c
                            b0, b1 = bank(), bank()

                            def pcol(hh, prv):
                                return 256 * prv + 128 * (hh // 2)
                            for par, bb in ((0, b0), (1, b1)):
                                n_ = 0
                                for hh in (par, par + 2):
                                    ch, p0 = hh // 2, 64 * (hh % 2)
                                    for prv, ksl in ((0, kcur), (1, kprv)):
                                        mm(psb(bb, pcol(hh, prv), pcol(hh, prv) + 128), kT[p0:p0 + 64, ch, ksl], qT[p0:p0 + 64, ch, qsl],
                                           n_ == 0, n_ == 3, reads=["bkT", "bqT"], writes=["ps%d" % bb])
                                        n_ += 1
                            act(eS[u][:, 0:512], psb(b0), AF.Exp, reads=["ps%d" % b0], writes=["beS%d" % u])
                            act(eS[u][:, 512:1024], psb(b1), AF.Exp, reads=["ps%d" % b1], writes=["beS%d" % u])
                            mask = mb0[:, :] if bq == 0 else cst[:, C_MB:C_MB + 1024]
                            tt("dve", pT[u][:], eS[u][:], mask, ALU.mult, reads=["beS%d" % u, "cst", "mb0"], writes=["bpT%d" % u])
                            bn, bd_ = bank(), bank()
                            first = True
                            for hh in range(4):
                                for (prv, vi) in ((0, vcur), (1, vprv)):
                                    off = 512 * (hh % 2) + pcol(hh, prv)
                                    mm(psb(bn, 128 * hh, 128 * hh + 128, 0, 64), vB[:, vi, 64 * hh:64 * hh + 64],
                                       pT[u][:, off:off + 128], first, hh == 3 and prv == 1,
                                       reads=["bv", "bpT%d" % u], writes=["ps%d" % bn])
                                    first = False
                            first = True
                            for hh in range(4):
                                for prv in (0, 1):
                                    off = 512 * (hh % 2) + pcol(hh, prv)
                                    mm(psb(bd_, 128 * hh, 128 * hh + 128, 0, 64), cs(C_ONES, 64),
                                       pT[u][:, off:off + 128], first, hh == 3 and prv == 1,
                                       reads=["cst", "bpT%d" % u], writes=["ps%d" % bd_])
                                    first = False
                            nview = psb(bn, 0, 512, 0, 64).rearrange("p (h t) -> p h t", h=4)
                            dview = psb(bd_, 0, 512, 0, 64).rearrange("p (h t) -> p h t", h=4)
                            tt("dve", accN[:, :, qsl], nview, accN[:, :, qsl], ALU.add, reads=["ps%d" % bn, "accN"], writes=["accN"])
                            tt("dve", accD[:, :, qsl], dview, accD[:, :, qsl], ALU.add, reads=["ps%d" % bd_, "accD"], writes=["accD"])
                for hh in range(4):
                    recip(accD[:, hh, :], accD[:, hh, :], reads=["accD"], writes=["accD"])
                    tt("dve", accN[:, hh, :], accN[:, hh, :], accD[:, hh, :], ALU.mult, reads=["accN", "accD"], writes=["accN"])
                ybst = [sb(s2, "ybst%d" % i, [64, TOK], BF16) for i in range(2)]
                for hh in range(4):
                    copy("pool", ybst[hh % 2][:], accN[:, hh, :], reads=["accN"], writes=["ybst%d" % (hh % 2)])
                    P.dma("sp", ybs.ap()[64 * hh:64 * hh + 64, :], ybst[hh % 2][:], reads=["ybst%d" % (hh % 2)], writes=["ybs"], dkey="ybs_w")

        def phase_a():
            with ExitStack() as s3:
                QT = sb(s3, "aQT", [64, S], BF16)
                KT = sb(s3, "aKT", [64, S], BF16)
                V = sb(s3, "aV", [128, 128, 64], BF16)
                eB = [sb(s3, "aE%d" % i, [128, 1536], F32) for i in range(2)]
                spB = [sb(s3, "aSP%d" % i, [128, 1536], BF16) for i in range(2)]
                wlB = [sb(s3, "aWL%d" % i, [128, 1536], BF16) for i in range(2)]
                acc = [sb(s3, "aAcc%d" % i, [128, 4, 64], F32) for i in range(2)]
                rtot = [sb(s3, "aRt%d" % i, [128, 4], F32) for i in range(2)]
                fac = [sb(s3, "aFac%d" % i, [128, 4], F32) for i in range(2)]
                accb = sb(s3, "aAccb", [128, 256], BF16)
                yaTs = [sb(s3, "aYaT%d" % i, [64, 512], BF16) for i in range(2)]
                def ld_q(e):
                    src = gat2.rearrange("(s r) t -> r s t", s=8)[bass.ds(P.val(e, "p64"), 64), :, :]
                    return e.dma_start(out=QT[:, :].rearrange("p (s t) -> p s t", s=8), in_=src)

                def ld_kk(e):
                    src = gat2.rearrange("(s r) t -> r s t", s=8)[bass.ds(P.val(e, "p64") + 512, 64), :, :]
                    return e.dma_start(out=KT[:, :].rearrange("p (s t) -> p s t", s=8), in_=src)
                P.add("sp", ld_q, reads=["gat"], writes=["aQT"], kind="dma", dkey="aQT")
                P.add("sp", ld_kk, reads=["gat"], writes=["aKT"], kind="dma", dkey="aKT")
                for s_ in range(8):
                    def ld_vv(e, s_=s_):
                        src = gat2[bass.ds(P.val(e, "p64") + (s_ * SND_ROWS + 1792), 64), :]
                        src = src.rearrange("a (b d) -> (a b) d", d=64).rearrange("(tb p) d -> p tb d", p=128)
                        return e.dma_start(out=V[:, 16 * s_:16 * s_ + 16, :], in_=src)
                    P.add(("sp", "act", "pool", "sp", "pool", "sp", "act", "pool")[s_], ld_vv, reads=["gat"], writes=["aV%d" % s_], kind="dma", dkey="aV%d" % s_)

                glist = []
                for qs in range(32):
                    tiles = []
                    for kb in range(4 * qs + 3, -1, -1):
                        j = kb - 4 * qs
                        tiles.append((kb, 128 * j if j >= 0 else 0, j >= 0))
                    groups = [tiles[0:2], tiles[2:4]] + [tiles[i:i + 3] for i in range(4, len(tiles), 3)]
                    for gi, grp in enumerate(groups):
                        glist.append((qs, gi, grp, gi == len(groups) - 1))

                def zb(gidx, ti, lo, hi):
                    return psb(3 * (gidx % 2) + ti, lo, hi)

                def zkey(gidx, ti):
                    return "ps%d" % (3 * (gidx % 2) + ti)

                def stage_a(gidx):
                    qs, gi, grp, last = glist[gidx]
                    u = gidx % 2
                    q0 = 512 * qs
                    full = all(not d for (_, _, d) in grp)
                    for ti, (kb, clo, dg) in enumerate(grp):
                        mm(zb(gidx, ti, clo, 512), KT[:, kb * 128:(kb + 1) * 128], QT[:, q0 + clo:q0 + 512], True, True,
                           reads=["aKT", "aQT"], writes=[zkey(gidx, ti)])
                    n = len(grp)
                    if full:
                        zall = ps[:, 1536 * u:1536 * u + 512 * n]
                        act(eB[u][:, 0:512 * n], zall, AF.Exp, reads=[zkey(gidx, t) for t in range(n)], writes=["aE%d" % u])
                        act(spB[u][:, 0:512 * n], eB[u][:, 0:512 * n], AF.Ln, reads=["aE%d" % u], writes=["aSP%d" % u], bias=1.0)
                    else:
                        for ti, (kb, clo, dg) in enumerate(grp):
                            act(eB[u][:, 512 * ti + clo:512 * ti + 512], zb(gidx, ti, clo, 512), AF.Exp,
                                reads=[zkey(gidx, ti)], writes=["aE%d" % u])
                        for ti, (kb, clo, dg) in enumerate(grp):
                            act(spB[u][:, 512 * ti + clo:512 * ti + 512], eB[u][:, 512 * ti + clo:512 * ti + 512], AF.Ln,
                                reads=["aE%d" % u], writes=["aSP%d" % u], bias=1.0)
                        for ti, (kb, clo, dg) in enumerate(grp):
                            if dg:
                                v_ = spB[u][:, 512 * ti + clo:512 * ti + clo + 128]
                                tt("pool", v_, v_, cs(C_MTRI), ALU.mult, reads=["aSP%d" % u, "cst"], writes=["aSP%d" % u])

                def stage_b(gidx):
                    qs, gi, grp, last = glist[gidx]
                    u = gidx % 2
                    a = qs % 2
                    q0 = 512 * qs
                    full = all(not d for (_, _, d) in grp)
                    n = len(grp)
                    if gi == 0:
                        memset("pool", acc[a][:], 0.0, writes=["aAcc%d" % a])
                        memset("pool", rtot[a][:], 0.0, writes=["aRt%d" % a])
                    for ti, (kb, clo, dg) in enumerate(grp):
                        mm(zb(gidx, ti, clo, 512), cs(C_NTRI), spB[u][:, 512 * ti + clo:512 * ti + 512], False, ti == 0,
                           reads=["aSP%d" % u, "cst"], writes=[zkey(gidx, ti)], sgc=True)
                        for t2 in range(ti):
                            clo2 = grp[t2][1]
                            mm(zb(gidx, ti, clo2, 512), cs(C_NONES), spB[u][:, 512 * t2 + clo2:512 * t2 + 512], False, t2 == ti - 1,
                               reads=["aSP%d" % u, "cst"], writes=[zkey(gidx, ti)], sgc=True)
                    if full:
                        zall = ps[:, 1536 * u:1536 * u + 512 * n]
                        act(wlB[u][:, 0:512 * n], zall, AF.Exp, reads=[zkey(gidx, t) for t in range(n)], writes=["aWL%d" % u])
                    else:
                        for ti, (kb, clo, dg) in enumerate(grp):
                            act(wlB[u][:, 512 * ti + clo:512 * ti + 512], zb(gidx, ti, clo, 512), AF.Exp,
                                reads=[zkey(gidx, ti)], writes=["aWL%d" % u])
                        for ti, (kb, clo, dg) in enumerate(grp):
                            if dg:
                                v_ = wlB[u][:, 512 * ti + clo:512 * ti + clo + 128]
                                tt("pool", v_, v_, cs(C_MTRI), ALU.mult, reads=["aWL%d" % u, "cst"], writes=["aWL%d" % u])
                    pvb = 6 + (gidx % 2)
                    pk = "ps%d" % pvb
                    first = True
                    cmin = min(clo for (_, clo, _) in grp) // 128
                    pvl = []
                    for ti, (kb, clo, dg) in enumerate(grp):
                        for c in range(clo // 128, 4):
                            pvl.append((psb(pvb, 64 * c, 64 * c + 64), wlB[u][:, 512 * ti + 128 * c:512 * ti + 128 * c + 128], V[:, kb, :],
                                        ["aWL%d" % u, "aV%d" % (kb // 16)]))
                            pvl.append((psb(pvb, 256 + 2 * c, 256 + 2 * c + 2), spB[u][:, 512 * ti + 128 * c:512 * ti + 128 * c + 128],
                                        cs(C_ONES, 2), ["aSP%d" % u, "cst"]))
                    for i_, (o_, l_, r_, rd_) in enumerate(pvl):
                        mm(o_, l_, r_, i_ == 0, i_ == len(pvl) - 1, reads=rd_, writes=[pk])
                    act(fac[a][:, cmin:4], rtot[a][:, cmin:4], AF.Exp, reads=["aRt%d" % a], writes=["aFac%d" % a], scale=-1.0)
                    for c in range(cmin, 4):
                        stt(acc[a][:, c, :], psb(pvb, 64 * c, 64 * c + 64), fac[a][:, c:c + 1], acc[a][:, c, :], ALU.mult, ALU.add,
                            reads=[pk, "aFac%d" % a, "aAcc%d" % a], writes=["aAcc%d" % a])
                    tt("dve", rtot[a][:, cmin:4], rtot[a][:, cmin:4], ps[:, pvb * 512 + 256 + 2 * cmin:pvb * 512 + 264:2], ALU.add,
                       reads=[pk, "aRt%d" % a], writes=["aRt%d" % a])
                    if last:
                        copy("dve", accb[:], acc[a][:].rearrange("p c d -> p (c d)"), reads=["aAcc%d" % a], writes=["aAccb"])
                        tb_ = 6 + ((gidx + 1) % 2)
                        for c in range(4):
                            mm(psb(tb_, 128 * c, 128 * c + 128, 0, 64), accb[:, 64 * c:64 * c + 64], cs(C_ID), c == 0, c == 3,
                               reads=["aAccb", "cst"], writes=["ps%d" % tb_])
                        copy("dve", yaTs[a][:], psb(tb_, 0, 512, 0, 64), reads=["ps%d" % tb_], writes=["aYaT%d" % a])
                        P.dma("sp", yas.ap()[:, q0:q0 + 512], yaTs[a][:], reads=["aYaT%d" % a], writes=["yas"], dkey="yas_w%d" % a)

                stage_a(0)
                for gidx in range(len(glist)):
                    if gidx + 1 < len(glist):
                        stage_a(gidx + 1)
                    stage_b(gidx)

        if stage == -3:
            ybs_in = din("ybs_in", [256, TOK])
            ycs_in = din("ycs_in", [256, TOK])
            yas_in = din("yas_in", [64, S])
            P.dma("pool", ybs.ap(), ybs_in, writes=["ybs"])
            P.dma("pool", ycs.ap(), ycs_in, writes=["ycs"])
            P.dma("pool", yas.ap(), yas_in, writes=["yas"])
        elif stage in (-1, -2):
            snd_in = din("snd_in", [SND_ROWS, 2048])
            qbs_in = din("qbs_in", [768, TOK])
            P.dma("pool", snd.ap(), snd_in, writes=["snd"])
            P.dma("pool", qbs.ap(), qbs_in, writes=["qbs"])
        else:
            phase13(1)
        if stage >= 3 or stage in (-1, -2):
            P.barrier()
            P.add("pool", lambda e: e.collective_compute("AllGather", ALU.bypass, replica_groups=[list(range(NCORES))],
                                                         ins=[snd.ap().opt()], outs=[gat.ap().opt()]),
                  reads=["snd"], writes=["gat"], kind="dma", dkey="cc1", dinc=1)
            P.barrier()
            if stage != -2:
                phase_b()
        if stage >= 4 or stage in (-2, -3):
            P.barrier()
            if stage != -3:
                phase_a()
            P.barrier()
            P.add("pool", lambda e: e.collective_compute("AllGather", ALU.bypass, replica_groups=[list(range(NCORES))],
                                                         ins=[yas.ap().opt()], outs=[yag.ap().opt()]),
                  reads=["yas"], writes=["yag"], kind="dma", dkey="cc2", dinc=1)
        if stage >= 5 or stage == -3:
            P.barrier()
            phase13(3)
        else:
            P.barrier()
            for kc in range(8):
                P.dma("sp", outT_d[kc * 128:(kc + 1) * 128, :], xT[:, kc, :], reads=["x0", "x1", "x2", "x3"], writes=["outT"], dkey="out_w")
        dbg = {}
        if stage < 99:
            dl = (("snd", snd), ("qbs", qbs), ("ycs", ycs), ("ybs", ybs), ("yas", yas))
            if stage == -3:
                dl = ()
            elif stage == -1:
                dl = (("ybs", ybs),)
            elif stage == -2:
                dl = (("yas", yas),)
            elif stage < 3:
                dl = dl[:3]
            elif stage < 4:
                dl = dl[:4]
            for name, t in dl:
                o = nc.dram_tensor("dbg_" + name, list(t.ap().shape), BF16, kind="ExternalOutput")
                P.barrier()
                P.dma("sp", o.ap(), t.ap(), reads=[name], writes=["dbg_" + name])
        P.emit(st)
    return nc


_CACHE = {}


def _prep_inputs(inputs):
    x = np.asarray(inputs["x"], np.float32)[0]
    mem = np.asarray(inputs["mem"], np.float32)[0]

    def col8(v):
        return np.ascontiguousarray(np.asarray(v, np.float32).reshape(-1, 128).T)

    vecs = np.zeros((128, 64), np.float32)
    vecs[:, 0:8] = col8(inputs["ffn1_norm"][0])
    vecs[:, 8:16] = col8(inputs["mix_norm"][0])
    vecs[:, 16:24] = col8(inputs["mem_norm"][0])
    vecs[:, 24:32] = col8(inputs["ffn2_norm"][0])
    vecs[:, 32:56] = col8(inputs["b_gate"][0])
    for i, k in enumerate(("qn_dsa", "kn_dsa", "qn_mem", "kn_mem")):
        vecs[:, 56 + i] = np.tile(np.asarray(inputs[k], np.float32)[0], 2)
    shared = {
        "memT": np.ascontiguousarray(mem.T),
        "f1w1": np.asarray(inputs["ffn1_w1"], np.float32)[0], "f1w3": np.asarray(inputs["ffn1_w3"], np.float32)[0],
        "f1w2": np.asarray(inputs["ffn1_w2"], np.float32)[0],
        "f2w1": np.asarray(inputs["ffn2_w1"], np.float32)[0], "f2w3": np.asarray(inputs["ffn2_w3"], np.float32)[0],
        "f2w2": np.asarray(inputs["ffn2_w2"], np.float32)[0],
        "w_in": np.asarray(inputs["w_in"], np.float32)[0], "w_mem_kv": np.asarray(inputs["w_mem_kv"], np.float32)[0],
        "w_sb": np.asarray(inputs["w_branch_sb"], np.float32)[0], "w_dsa": np.asarray(inputs["w_branch_dsa"], np.float32)[0],
        "w_mem": np.asarray(inputs["w_branch_mem"], np.float32)[0],
        "w_gate": np.asarray(inputs["w_gate"], np.float32)[0], "w_out": np.asarray(inputs["w_out"], np.float32)[0],
        "vecs": vecs,
    }
    in_maps = []
    for c in range(NCORES):
        cst, mb0, rt, rope = _consts(c)
        m = dict(shared)
        m["xT"] = np.ascontiguousarray(x[c * TOK:(c + 1) * TOK, :].T)
        m["cst"] = cst
        m["mb0"] = mb0
        m["rt"] = rt
        m["rope"] = rope
        in_maps.append(m)
    return in_maps


def run(inputs, stage=99):
    if stage not in _CACHE:
        _CACHE[stage] = build(stage)
    nc = _CACHE[stage]
    in_maps = _prep_inputs(inputs)
    res = run_bass_kernel_spmd(nc, in_maps, core_ids=list(range(NCORES)))
    return res


def kernel(**inputs):
    res = run(inputs, 99)
    out = np.empty((1, S, D), np.float32)
    for c in range(NCORES):
        out[0, c * TOK:(c + 1) * TOK, :] = res.results[c]["outT"].T
    return out
```
